# Optimizing a Trainium2 kernel written in Bass

```python
import math
import jax, jax.numpy as jnp
from jax import lax
import numpy as np

D_MODEL = 1024
BATCH = 8
SEQ = 2048
DEPTH = 4
DEC_BATCH = 128
DEC_SEQ = 8
PAST_LEN = 16384
PAGE_SIZE = 128

N_EVEN = (DEPTH + 1) // 2
N_ODD = DEPTH // 2
CONV_W = 4
NORM_EPS = 1e-6
RWKV_HEADS = 8
RWKV_HD = 64
RWKV_W = RWKV_HEADS * RWKV_HD
DECAY_LORA = 64
AAA_LORA = 64
GATE_LORA = 128
RWKV_PROJ = 3 * RWKV_W + DECAY_LORA + AAA_LORA + GATE_LORA
RWKV_GN_EPS = 64e-5
GDN_HEADS = 4
GDN_HD = 128
GDN_W = GDN_HEADS * GDN_HD
GDN_QKV = 3 * GDN_W
GDN_PROJ = GDN_QKV + GDN_W + 2 * GDN_HEADS
GDN_CHUNK = 64
EVEN_PROJ = RWKV_PROJ + GDN_PROJ
MIX_W = RWKV_W + GDN_W
LRU_W = D_MODEL
LRU_HEADS = 8
LRU_BW = LRU_W // LRU_HEADS
LRU_C = 8.0
D_FF = 4 * D_MODEL

kernel_name = 'rwkv7_gdn_rglru_hybrid_step'


def rmsnorm(x, w, eps=NORM_EPS):
    xf = x.astype(jnp.float32)
    y = xf * lax.rsqrt(jnp.mean(xf * xf, -1, keepdims=True) + eps)
    return (y * w.astype(jnp.float32)).astype(x.dtype)


def l2norm(x, eps=1e-6):
    xf = x.astype(jnp.float32)
    return xf * lax.rsqrt(jnp.sum(xf * xf, -1, keepdims=True) + eps)


def causal_conv(x, buf, w):
    T = x.shape[1]
    xp = jnp.concatenate([buf.astype(x.dtype), x], axis=1)
    y = sum(w[j] * xp[:, j:j + T] for j in range(CONV_W))
    return y, xp[:, T:]


def wkv7_scan(r, w, k, v, a, b, S0):
    def step(S, inp):
        r_t, w_t, k_t, v_t, a_t, b_t = inp
        sa = jnp.einsum('bhvk,bhk->bhv', S, a_t)
        S = S * w_t[:, :, None, :] + sa[..., None] * b_t[:, :, None, :] + v_t[..., None] * k_t[:, :, None, :]
        return S, jnp.einsum('bhvk,bhk->bhv', S, r_t)
    xs = tuple(jnp.swapaxes(t, 0, 1) for t in (r, w, k, v, a, b))
    S, y = lax.scan(step, S0, xs)
    return jnp.swapaxes(y, 0, 1), S


def rwkv7_mix(p, prev, mu, w0, w2, a0, a2, g2, k_k, k_a, r_k, ln_w, ln_b, S0):
    B, T, _ = p.shape
    f32 = jnp.float32
    shifted = jnp.concatenate([prev[:, None].astype(p.dtype), p[:, :-1]], axis=1)
    xs = p + mu * (shifted - p)
    o1, o2, o3 = RWKV_W, 2 * RWKV_W, 3 * RWKV_W
    o4 = o3 + DECAY_LORA
    o5 = o4 + AAA_LORA
    r, k, v = xs[..., :o1], xs[..., o1:o2], xs[..., o2:o3]
    xw, xa, xg = xs[..., o3:o4], xs[..., o4:o5], xs[..., o5:]
    w_log = -jax.nn.softplus(-(w0 + jnp.tanh(xw) @ w2).astype(f32)) - 0.5
    decay = jnp.exp(-jnp.exp(w_log))
    a = jax.nn.sigmoid((a0 + xa @ a2).astype(f32))
    g = jax.nn.sigmoid(xg) @ g2
    heads = lambda t: t.astype(f32).reshape(B, T, RWKV_HEADS, RWKV_HD)
    hk = lambda t: t.astype(f32).reshape(RWKV_HEADS, RWKV_HD)
    r, k, v, decay, a = heads(r), heads(k), heads(v), heads(decay), heads(a)
    kk = l2norm(k * hk(k_k))
    k = k * (1.0 + (a - 1.0) * hk(k_a))
    y, S = wkv7_scan(r, decay, k, v, -kk, kk * a, S0.astype(f32))
    mean = jnp.mean(y, -1, keepdims=True)
    var = jnp.mean(jnp.square(y - mean), -1, keepdims=True)
    y = (y - mean) * lax.rsqrt(var + RWKV_GN_EPS) * hk(ln_w) + hk(ln_b)
    y = y + jnp.sum(r * k * r_k.astype(f32), -1, keepdims=True) * v
    out = y.reshape(B, T, RWKV_W).astype(p.dtype) * g
    return out, S, p[:, -1]


def gated_delta_chunked(q, k, v, g, beta, S0):
    B, T, H, _ = q.shape
    C = min(GDN_CHUNK, T)
    pad = (-T) % C
    N = (T + pad) // C

    def chunks(t):
        t = jnp.pad(t, [(0, 0), (0, pad)] + [(0, 0)] * (t.ndim - 2))
        t = t.reshape((B, N, C) + t.shape[2:])
        perm = (1, 0, 3, 2, 4) if t.ndim == 5 else (1, 0, 3, 2)
        return t.transpose(perm)

    q, k, v, g, beta = chunks(q), chunks(k), chunks(v), chunks(g), chunks(beta)
    gc = jnp.cumsum(g, axis=-1)
    kb = k * beta[..., None]
    vb = v * beta[..., None]
    idx = jnp.arange(C)
    incl = idx[:, None] >= idx[None, :]
    strict = idx[:, None] > idx[None, :]
    diff = gc[..., :, None] - gc[..., None, :]
    decay = jnp.where(incl, jnp.exp(jnp.where(incl, diff, 0.0)), 0.0)
    L = jnp.where(strict, jnp.einsum('nbhid,nbhjd->nbhij', kb, k) * decay, 0.0)
    eye = jnp.broadcast_to(jnp.eye(C, dtype=L.dtype), L.shape)
    Tinv = lax.linalg.triangular_solve(eye + L, eye, left_side=True, lower=True, unit_diagonal=True)
    u = jnp.einsum('nbhij,nbhjd->nbhid', Tinv, vb)
    w = jnp.einsum('nbhij,nbhjd->nbhid', Tinv, kb * jnp.exp(gc)[..., None])
    qk = jnp.where(incl, jnp.einsum('nbhid,nbhjd->nbhij', q, k) * decay, 0.0)

    def step(S, inp):
        q_c, k_c, u_c, w_c, qk_c, gc_c = inp
        v_new = u_c - jnp.einsum('bhck,bhkv->bhcv', w_c, S)
        o = jnp.einsum('bhck,bhkv->bhcv', q_c * jnp.exp(gc_c)[..., None], S) + jnp.einsum('bhij,bhjv->bhiv', qk_c, v_new)
        g_last = gc_c[..., -1:]
        S = S * jnp.exp(g_last)[..., None] + jnp.einsum('bhck,bhcv->bhkv', k_c * jnp.exp(g_last - gc_c)[..., None], v_new)
        return S, o

    S, o = lax.scan(step, S0, (q, k, u, w, qk, gc))
    o = o.transpose(1, 0, 3, 2, 4).reshape(B, N * C, H, -1)[:, :T]
    return o, S


def gdn_mix(p, buf, conv_w, A_log, dt_bias, norm_w, S0):
    B, T, _ = p.shape
    f32 = jnp.float32
    qkv, new_buf = causal_conv(p[..., :GDN_QKV], buf, conv_w)
    qkv = jax.nn.silu(qkv)
    z = p[..., GDN_QKV:GDN_QKV + GDN_W]
    b_raw = p[..., GDN_QKV + GDN_W:GDN_QKV + GDN_W + GDN_HEADS]
    a_raw = p[..., GDN_QKV + GDN_W + GDN_HEADS:]
    heads = lambda t: t.reshape(B, T, GDN_HEADS, GDN_HD)
    q = l2norm(heads(qkv[..., :GDN_W])) * (GDN_HD ** -0.5)
    k = l2norm(heads(qkv[..., GDN_W:2 * GDN_W]))
    v = heads(qkv[..., 2 * GDN_W:]).astype(f32)
    beta = jax.nn.sigmoid(b_raw.astype(f32))
    g = -jnp.exp(A_log.astype(f32)) * jax.nn.softplus(a_raw.astype(f32) + dt_bias.astype(f32))
    o, S = gated_delta_chunked(q, k, v, g, beta, S0.astype(f32))
    o = o * lax.rsqrt(jnp.mean(o * o, -1, keepdims=True) + NORM_EPS) * norm_w.astype(f32)
    o = o * jax.nn.silu(heads(z).astype(f32))
    return o.reshape(B, T, GDN_W).astype(p.dtype), S, new_buf


def lru_mix(xn, w_in, conv_w, conv_b, wa, ba, wi, bi, L, h0, buf):
    B, T, _ = xn.shape
    f32 = jnp.float32
    proj = xn @ w_in
    gate_branch, xb = proj[..., :LRU_W], proj[..., LRU_W:]
    xc, new_buf = causal_conv(xb, buf, conv_w)
    xc = xc + conv_b
    xh = xc.reshape(B, T, LRU_HEADS, LRU_BW)
    r = jax.nn.sigmoid((jnp.einsum('bthi,hij->bthj', xh, wa) + ba).astype(f32))
    i = jax.nn.sigmoid((jnp.einsum('bthi,hij->bthj', xh, wi) + bi).astype(f32))
    log_a = -LRU_C * r * jax.nn.softplus(-L.astype(f32))
    mult = jnp.sqrt(-jnp.expm1(2.0 * log_a))
    a = jnp.exp(log_a).reshape(B, T, LRU_W)
    bvals = (mult * i * xh.astype(f32)).reshape(B, T, LRU_W)
    bvals = bvals.at[:, 0].add(a[:, 0] * h0.astype(f32))

    def comb(c1, c2):
        a1, b1 = c1
        a2, b2 = c2
        return a1 * a2, a2 * b1 + b2

    _, h = lax.associative_scan(comb, (a, bvals), axis=1)
    y = h.astype(xn.dtype) * jax.nn.gelu(gate_branch, approximate=True)
    return y, h[:, -1], new_buf


def sq_relu_mlp(x, up, down):
    return jnp.square(jax.nn.relu(x @ up)) @ down


def setup_inputs(seed: int = 0) -> dict:
    key = jax.random.key(seed)
    ks = jax.random.split(key, 48)
    f32 = jnp.float32

    def nrm(i, shape, scale):
        return jax.random.normal(ks[i], shape, f32) * scale

    def uni(i, shape, lo, hi):
        return jax.random.uniform(ks[i], shape, f32, lo, hi)

    dt = jnp.exp(uni(26, (N_EVEN, GDN_HEADS), math.log(1e-3), math.log(1e-1)))
    a_base = uni(36, (N_ODD, LRU_HEADS, LRU_BW), 0.9, 0.999)
    s = a_base ** (1.0 / LRU_C)
    return {
        'x_prompt': nrm(0, (BATCH, SEQ, D_MODEL), 1.0),
        'x_sample': nrm(1, (DEC_BATCH, DEC_SEQ, D_MODEL), 1.0),
        'state_rwkv': nrm(2, (N_EVEN, DEC_BATCH, RWKV_HEADS, RWKV_HD, RWKV_HD), 0.3),
        'state_rwkv_shift': nrm(3, (N_EVEN, DEC_BATCH, RWKV_PROJ), 1.0),
        'state_gdn': nrm(4, (N_EVEN, DEC_BATCH, GDN_HEADS, GDN_HD, GDN_HD), 0.3),
        'state_gdn_conv': nrm(5, (N_EVEN, DEC_BATCH, CONV_W - 1, GDN_QKV), 1.0),
        'state_lru': nrm(6, (N_ODD, DEC_BATCH, LRU_W), 0.5),
        'state_lru_conv': nrm(7, (N_ODD, DEC_BATCH, CONV_W - 1, LRU_W), 1.0),
        'norm_mix': 1.0 + nrm(8, (DEPTH, D_MODEL), 0.02),
        'norm_mlp': 1.0 + nrm(9, (DEPTH, D_MODEL), 0.02),
        'norm_final': 1.0 + nrm(10, (D_MODEL,), 0.02),
        'w_in_even': nrm(11, (N_EVEN, D_MODEL, EVEN_PROJ), D_MODEL ** -0.5),
        'w_out_even': nrm(12, (N_EVEN, MIX_W, D_MODEL), MIX_W ** -0.5),
        'rwkv_mu': uni(13, (N_EVEN, RWKV_PROJ), 0.0, 1.0),
        'rwkv_w0': uni(14, (N_EVEN, RWKV_W), -6.0, 0.0),
        'rwkv_w2': nrm(15, (N_EVEN, DECAY_LORA, RWKV_W), 0.5 * DECAY_LORA ** -0.5),
        'rwkv_a0': nrm(16, (N_EVEN, RWKV_W), 0.1),
        'rwkv_a2': nrm(17, (N_EVEN, AAA_LORA, RWKV_W), 0.5 * AAA_LORA ** -0.5),
        'rwkv_g2': nrm(18, (N_EVEN, GATE_LORA, RWKV_W), GATE_LORA ** -0.5),
        'rwkv_k_k': 0.85 + nrm(19, (N_EVEN, RWKV_W), 0.05),
        'rwkv_k_a': 1.0 + nrm(20, (N_EVEN, RWKV_W), 0.05),
        'rwkv_r_k': nrm(21, (N_EVEN, RWKV_HEADS, RWKV_HD), 0.1),
        'rwkv_ln_w': 1.0 + nrm(22, (N_EVEN, RWKV_W), 0.02),
        'rwkv_ln_b': nrm(23, (N_EVEN, RWKV_W), 0.02),
        'gdn_conv_w': nrm(24, (N_EVEN, CONV_W, GDN_QKV), CONV_W ** -0.5),
        'gdn_A_log': jnp.log(uni(25, (N_EVEN, GDN_HEADS), 1.0, 16.0)),
        'gdn_dt_bias': dt + jnp.log(-jnp.expm1(-dt)),
        'gdn_norm_w': 1.0 + nrm(27, (N_EVEN, GDN_HD), 0.02),
        'w_in_odd': nrm(28, (N_ODD, D_MODEL, 2 * LRU_W), D_MODEL ** -0.5),
        'w_out_odd': nrm(29, (N_ODD, LRU_W, D_MODEL), LRU_W ** -0.5),
        'lru_conv_w': nrm(30, (N_ODD, CONV_W, LRU_W), CONV_W ** -0.5),
        'lru_conv_b': nrm(31, (N_ODD, LRU_W), 0.02),
        'lru_wa': nrm(32, (N_ODD, LRU_HEADS, LRU_BW, LRU_BW), LRU_BW ** -0.5),
        'lru_ba': nrm(33, (N_ODD, LRU_HEADS, LRU_BW), 0.1),
        'lru_wi': nrm(34, (N_ODD, LRU_HEADS, LRU_BW, LRU_BW), LRU_BW ** -0.5),
        'lru_bi': nrm(35, (N_ODD, LRU_HEADS, LRU_BW), 0.1),
        'lru_L': jnp.log(s) - jnp.log1p(-s),
        'mlp_up': nrm(37, (DEPTH, D_MODEL, D_FF), D_MODEL ** -0.5),
        'mlp_down': nrm(38, (DEPTH, D_FF, D_MODEL), D_FF ** -0.5),
    }


def reference(x_prompt, x_sample, state_rwkv, state_rwkv_shift, state_gdn, state_gdn_conv, state_lru, state_lru_conv,
              norm_mix, norm_mlp, norm_final, w_in_even, w_out_even,
              rwkv_mu, rwkv_w0, rwkv_w2, rwkv_a0, rwkv_a2, rwkv_g2, rwkv_k_k, rwkv_k_a, rwkv_r_k, rwkv_ln_w, rwkv_ln_b,
              gdn_conv_w, gdn_A_log, gdn_dt_bias, gdn_norm_w,
              w_in_odd, w_out_odd, lru_conv_w, lru_conv_b, lru_wa, lru_ba, lru_wi, lru_bi, lru_L,
              mlp_up, mlp_down):

    def trunk(x, st_rwkv, st_shift, st_gdn, st_gconv, st_lru, st_lconv):
        h = x
        n_rwkv, n_shift, n_gdn, n_gconv, n_lru, n_lconv = [], [], [], [], [], []
        for l in range(DEPTH):
            i = l // 2
            xn = rmsnorm(h, norm_mix[l])
            if l % 2 == 0:
                proj = xn @ w_in_even[i]
                ya, S_a, sh = rwkv7_mix(proj[..., :RWKV_PROJ], st_shift[i], rwkv_mu[i], rwkv_w0[i], rwkv_w2[i],
                                        rwkv_a0[i], rwkv_a2[i], rwkv_g2[i], rwkv_k_k[i], rwkv_k_a[i], rwkv_r_k[i],
                                        rwkv_ln_w[i], rwkv_ln_b[i], st_rwkv[i])
                yb, S_b, cb = gdn_mix(proj[..., RWKV_PROJ:], st_gconv[i], gdn_conv_w[i], gdn_A_log[i],
                                      gdn_dt_bias[i], gdn_norm_w[i], st_gdn[i])
                h = h + jnp.concatenate([ya, yb], axis=-1) @ w_out_even[i]
                n_rwkv.append(S_a.astype(x.dtype))
                n_shift.append(sh)
                n_gdn.append(S_b.astype(x.dtype))
                n_gconv.append(cb)
            else:
                yc, hl, cl = lru_mix(xn, w_in_odd[i], lru_conv_w[i], lru_conv_b[i], lru_wa[i], lru_ba[i],
                                     lru_wi[i], lru_bi[i], lru_L[i], st_lru[i], st_lconv[i])
                h = h + yc @ w_out_odd[i]
                n_lru.append(hl.astype(x.dtype))
                n_lconv.append(cl)
            h = h + sq_relu_mlp(rmsnorm(h, norm_mlp[l]), mlp_up[l], mlp_down[l])
        y = rmsnorm(h, norm_final)
        return (y, jnp.stack(n_rwkv), jnp.stack(n_shift), jnp.stack(n_gdn), jnp.stack(n_gconv),
                jnp.stack(n_lru), jnp.stack(n_lconv))

    Bp = x_prompt.shape[0]
    dt = x_prompt.dtype
    y_prompt, rwkv_p, shift_p, gdn_p, gconv_p, lru_p, lconv_p = trunk(
        x_prompt,
        jnp.zeros((N_EVEN, Bp, RWKV_HEADS, RWKV_HD, RWKV_HD), dt),
        jnp.zeros((N_EVEN, Bp, RWKV_PROJ), dt),
        jnp.zeros((N_EVEN, Bp, GDN_HEADS, GDN_HD, GDN_HD), dt),
        jnp.zeros((N_EVEN, Bp, CONV_W - 1, GDN_QKV), dt),
        jnp.zeros((N_ODD, Bp, LRU_W), dt),
        jnp.zeros((N_ODD, Bp, CONV_W - 1, LRU_W), dt))
    y_sample, rwkv_s, shift_s, gdn_s, gconv_s, lru_s, lconv_s = trunk(
        x_sample, state_rwkv, state_rwkv_shift, state_gdn, state_gdn_conv, state_lru, state_lru_conv)
    return (y_prompt, y_sample, rwkv_p, rwkv_s, shift_p, shift_s, gdn_p, gdn_s, gconv_p, gconv_s, lru_p, lru_s, lconv_p, lconv_s)
```

```python
import numpy as np
import concourse.bass as bass
import concourse.mybir as mybir
from concourse.bass_utils import run_bass_kernel_spmd
from contextlib import ExitStack

F32 = mybir.dt.float32
F32R = mybir.dt.float32r
AF = mybir.ActivationFunctionType
ALU = mybir.AluOpType
AX = mybir.AxisListType


def rd(ap):
    if ap is None or isinstance(ap, (int, float)):
        return ap
    if ap.dtype == F32R:
        return ap.bitcast(F32)
    return ap


def r32(ap):
    if ap.dtype == F32:
        return ap.bitcast(F32R)
    return ap


class Prog:
    ENG = ('pe', 'act', 'dve', 'pool', 'sp')
    EPOCH = 30000
    NDMA = 20

    def __init__(self, nc, stack):
        self.nc = nc
        self.stack = stack
        self.q = {e: [] for e in self.ENG}
        self.sems = []
        self.owner = []
        self.cur = {}
        self.cnt = {}
        for e in self.ENG:
            self.cur[e] = self.new_sem('s_' + e, e)
            self.cnt[e] = 0
        self.dsem = [self.new_sem('d%d' % i, 'dma') for i in range(self.NDMA)]
        self.dval = [0] * self.NDMA
        self.dnext = 0
        self.lastw = {}
        self.readers = {}
        self.seen = {e: {} for e in self.ENG}
        self.out_tokens = []
        self.ninstr = 0

    def new_sem(self, name, owner):
        h = self.stack.enter_context(self.nc.semaphore(name))
        self.sems.append(h)
        self.owner.append(owner)
        return len(self.sems) - 1

    @staticmethod
    def key(ap):
        try:
            if str(ap.space).lower().find('dram') >= 0 or str(ap.space).lower().find('hbm') >= 0:
                return None
        except Exception:
            pass
        return ap.tensor.name

    def emit(self, eng, fn, reads, writes, dma=False, is_out=False):
        rk = [k for k in (self.key(a) for a in reads if a is not None and not isinstance(a, (int, float))) if k]
        wk = [k for k in (self.key(a) for a in writes) if k]
        waits = {}
        seen = self.seen[eng]

        def need(tok, raw):
            s, v = tok
            if self.owner[s] == eng and not raw:
                return
            if self.owner[s] == eng and eng == 'pe':
                return
            if seen.get(s, 0) >= v:
                return
            if waits.get(s, 0) < v:
                waits[s] = v

        for k in rk:
            t = self.lastw.get(k)
            if t:
                need(t, True)
        for k in wk:
            t = self.lastw.get(k)
            if t:
                need(t, False)
            for s, v in self.readers.get(k, {}).items():
                need((s, v), False)
        if dma:
            di = self.dnext
            self.dnext = (self.dnext + 1) % self.NDMA
            s = self.dsem[di]
            if self.dval[di] > 0:
                need((s, self.dval[di]), True)
            self.dval[di] += 16
            tok = (s, self.dval[di])
            inc = 16
        else:
            if self.cnt[eng] >= self.EPOCH:
                self.cur[eng] = self.new_sem('s_%s_%d' % (eng, len(self.sems)), eng)
                self.cnt[eng] = 0
            self.cnt[eng] += 1
            tok = (self.cur[eng], self.cnt[eng])
            inc = 1
        for s, v in waits.items():
            seen[s] = v
        wl = [(self.sems[s], v) for s, v in waits.items()]
        semh = self.sems[tok[0]]
        self.q[eng].append((wl, fn, semh, inc))
        self.ninstr += 1
        for k in rk:
            self.readers.setdefault(k, {})
            d = self.readers[k]
            if d.get(tok[0], 0) < tok[1]:
                d[tok[0]] = tok[1]
        for k in wk:
            self.lastw[k] = tok
            self.readers[k] = {}
        if is_out:
            self.out_tokens.append(tok)
        return tok

    def mm(self, out, lhsT, rhs, start=True, stop=True, r32=False):
        if r32:
            l2, r2 = globals()['r32'](lhsT), globals()['r32'](rhs)
        else:
            l2, r2 = rd(lhsT), rd(rhs)
        self.emit('pe', lambda e: e.matmul(out, l2, r2, start=start, stop=stop), [lhsT, rhs], [out])

    def tr(self, out, in_, ident):
        i2, d2 = rd(in_), rd(ident)
        self.emit('pe', lambda e: e.transpose(out, i2, d2), [in_, ident], [out])

    def act(self, out, in_, func, bias=None, scale=1.0, accum_out=None, eng='act'):
        kw = {}
        if bias is not None:
            kw['bias'] = bias
        if accum_out is not None:
            kw['accum_out'] = accum_out
        rd = [in_, bias if not isinstance(bias, (int, float)) else None,
              scale if not isinstance(scale, (int, float)) else None]
        if bias is not None:
            kw['bias'] = globals()['rd'](bias)
        scale = globals()['rd'](scale)
        wr = [out] + ([accum_out] if accum_out is not None else [])
        i2 = globals()['rd'](in_)
        self.emit('act', lambda e: e.activation(out, i2, func, scale=scale, **kw), rd, wr)

    def tt(self, out, in0, in1, op, eng='dve'):
        a, b = rd(in0), rd(in1)
        self.emit(eng, lambda e: e.tensor_tensor(out, a, b, op), [in0, in1], [out])

    def ts(self, out, in0, s1, s2=None, op0=ALU.mult, op1=None, eng='dve'):
        rl = [in0] + [s for s in (s1, s2) if s is not None and not isinstance(s, (int, float))]
        a, s1, s2 = rd(in0), rd(s1), rd(s2)
        if op1 is None:
            self.emit(eng, lambda e: e.tensor_scalar(a, a, s1, None, op0) if False else e.tensor_scalar(out, a, s1, None, op0), rl, [out])
        else:
            self.emit(eng, lambda e: e.tensor_scalar(out, a, s1, s2, op0, op1), rl, [out])

    def stt(self, out, in0, scalar, in1, op0, op1):
        rl = [in0, in1] + ([scalar] if not isinstance(scalar, (int, float)) else [])
        a, b, sc = rd(in0), rd(in1), rd(scalar)
        self.emit('dve', lambda e: e.scalar_tensor_tensor(out, a, sc, b, op0, op1), rl, [out])

    def copy(self, out, in_, eng='dve'):
        if eng == 'act':
            self.act(out, in_, AF.Copy)
        else:
            i2 = rd(in_)
            self.emit(eng, lambda e: e.tensor_copy(out, i2), [in_], [out])

    def recip(self, out, in_):
        i2 = rd(in_)
        self.emit('dve', lambda e: e.reciprocal(out, i2), [in_], [out])

    def memset(self, ap, val, eng='pool'):
        self.emit(eng, lambda e: e.memset(ap, val), [], [ap])

    def reduce(self, out, in_, op=ALU.add, axis=AX.X):
        i2 = rd(in_)
        self.emit('dve', lambda e: e.tensor_reduce(out, i2, axis, op), [in_], [out])

    def scan(self, out, d0, d1, init, op0, op1):
        rl = [d0, d1] + ([init] if not isinstance(init, (int, float)) else [])
        a, b, c = rd(d0), rd(d1), rd(init)
        self.emit('dve', lambda e: e.tensor_tensor_scan(out, a, b, c, op0, op1), rl, [out])

    def dma(self, out, in_, q='sp', is_out=False, slow=False, cast=False):
        if out.dtype == F32R and not cast:
            out = out.bitcast(F32)
        if slow:
            self.emit(q, lambda e: e.dma_start(out=out, in_=in_, allow_slow_non_contiguous=True), [in_], [out],
                      dma=True, is_out=is_out)
        else:
            self.emit(q, lambda e: e.dma_start(out=out, in_=in_), [in_], [out], dma=True, is_out=is_out)

    def finish(self):
        fin = {}
        for s, v in self.out_tokens:
            fin[s] = max(fin.get(s, 0), v)
        for i, s in enumerate(self.dsem):
            if self.dval[i] > 0:
                fin[s] = max(fin.get(s, 0), self.dval[i])
        finl = [(self.sems[s], v) for s, v in fin.items()]
        q = self.q

        def run(e, lst, final=None):
            for wl, fn, semh, inc in lst:
                for sh, v in wl:
                    e.wait_ge(sh, v)
                fn(e).then_inc(semh, inc)
            if final:
                for sh, v in final:
                    e.wait_ge(sh, v)

        with self.nc.Block() as block:
            if q['sp'] or finl:
                @block.sync
                def _(e):
                    run(e, q['sp'], finl)
            if q['pe']:
                @block.tensor
                def _(e):
                    run(e, q['pe'])
            if q['act']:
                @block.scalar
                def _(e):
                    run(e, q['act'])
            if q['dve']:
                @block.vector
                def _(e):
                    run(e, q['dve'])
            if q['pool']:
                @block.gpsimd
                def _(e):
                    run(e, q['pool'])

TTP = 256
NTP = 2048 // TTP
DECAY_C = 0.6065306597126334


def R(ap):
    return ap.bitcast(F32R)


class Cfg:
    def __init__(self, kind, idx):
        self.kind = kind
        self.idx = idx
        if kind == 'p':
            self.NS, self.T, self.C, self.L = 1, TTP, 32, 5
            self.chunks = [(32 * q, 32) for q in range(TTP // 32)]
        else:
            self.NS, self.T, self.C, self.L = 16, 8, 8, 3
            self.chunks = [(8 * s, 8) for s in range(16)]
        self.TT = self.NS * self.T
        self.first = (kind == 'p' and idx == 0)
        self.last = (kind == 'p' and idx == NTP - 1)


def build_program():
    nc = bass.Bass("TRN2", target_bir_lowering=False)
    D = {}

    def din(name, shape):
        D[name] = nc.dram_tensor(name, list(shape), F32, kind="ExternalInput").ap()

    def dout(name, shape):
        D[name] = nc.dram_tensor(name, list(shape), F32, kind="ExternalOutput").ap()

    din('xp', [2048, 1024]); din('xs', [128, 1024])
    din('st_rwkv', [2, 16, 8, 64, 64]); din('st_shift', [2, 16, 1792]); din('st_gdn', [2, 16, 4, 128, 128])
    din('st_gconv', [2, 16, 3, 1536]); din('st_lru', [2, 16, 1024]); din('st_lconv', [2, 16, 3, 1024])
    din('norm_mix', [4, 1024]); din('norm_mlp', [4, 1024]); din('norm_final', [1024])
    din('w_in_even', [2, 1024, 3848]); din('w_out_even', [2, 1024, 1024])
    din('rwkv_mu', [2, 1792]); din('rwkv_w0', [2, 512]); din('rwkv_w2', [2, 64, 512]); din('rwkv_a0', [2, 512])
    din('rwkv_a2', [2, 64, 512]); din('rwkv_g2', [2, 128, 512]); din('rwkv_k_k', [2, 512]); din('rwkv_k_a', [2, 512])
    din('rwkv_r_k', [2, 512]); din('rwkv_ln_w', [2, 512]); din('rwkv_ln_b', [2, 512])
    din('gdn_conv_w', [2, 4, 1536]); din('gdn_A_log', [2, 4]); din('gdn_dt_bias', [2, 4]); din('gdn_norm_w', [2, 128])
    din('w_in_odd', [2, 1024, 2048]); din('w_out_odd', [2, 1024, 1024]); din('lru_conv_w', [2, 4, 1024])
    din('lru_conv_b', [2, 1024]); din('lru_wa', [2, 8, 128, 128]); din('lru_ba', [2, 1024]); din('lru_wi', [2, 8, 128, 128])
    din('lru_bi', [2, 1024]); din('lru_L', [2, 1024]); din('mlp_up', [4, 1024, 4096]); din('mlp_down', [4, 4096, 1024])
    dout('yp', [2048, 1024]); dout('ys', [128, 1024])
    dout('rwkv_p', [2, 8, 64, 64]); dout('rwkv_s', [2, 16, 8, 64, 64]); dout('shift_p', [2, 1792]); dout('shift_s', [2, 16, 1792])
    dout('gdn_p', [2, 4, 128, 128]); dout('gdn_s', [2, 16, 4, 128, 128]); dout('gconv_p', [2, 3, 1536]); dout('gconv_s', [2, 16, 3, 1536])
    dout('lru_p', [2, 1024]); dout('lru_s', [2, 16, 1024]); dout('lconv_p', [2, 3, 1024]); dout('lconv_s', [2, 16, 3, 1024])

    with ExitStack() as st:
        P = Prog(nc, st)

        def sb(name, shape, dt=F32):
            return st.enter_context(nc.sbuf_tensor(name, list(shape), dt))

        PS = [st.enter_context(nc.psum_tensor("ps%d" % i, [128, 512], F32)) for i in range(8)]
        psi = [0]

        def nps():
            psi[0] = (psi[0] + 1) % 8
            return PS[psi[0]]

        evi = [0]

        def evac(out, in_):
            evi[0] ^= 1
            P.copy(out, in_, eng='act' if evi[0] else 'dve')

        X = sb("X", [128, 8, TTP])
        XN = sb("XN", [128, 8, TTP])
        RS = sb("RS", [128, TTP])
        NW = 3
        WR = [sb("WR%d" % i, [128, 4096]) for i in range(NW)]
        B = [sb("B%d" % i, [128, 4, 264]) for i in range(12)]
        STGI = sb("STGI", [128, 1792])
        STGO = sb("STGO", [128, 1536])
        LW = sb("LW", [128, 3, 512])
        WG = sb("WG", [128, 2, 8, 128])
        ident = sb("ident", [128, 128]); ones32 = sb("ones32", [128, 128]); ones = sb("ones", [128, 128])
        blk64 = sb("blk64", [128, 128]); blk32 = sb("blk32", [128, 128])
        MSK2 = sb("MSK2", [128, 2, 128]); NMSK2 = sb("NMSK2", [128, 2, 128]); MLS = sb("MLS", [128, 128])
        cst = sb("cst", [128, 8])
        SEL = sb("SEL", [8, 4, 128]); SEL32 = sb("SEL32", [8, 4, 128])
        zeros32 = sb('zeros32', [128, 512])
        P.memset(zeros32[:], 0.0)
        P.memset(ones32[:], 1.0)
        P.copy(R(ones[:]), ones32[:])
        P.memset(ident[:], 1.0)
        P.emit('pool', lambda e: e.affine_select(ident[:], ident[:], [[-1, 128]], ALU.is_equal, 0.0, base=0,
                                                 channel_multiplier=1), [ident[:]], [ident[:]])
        P.memset(blk32[:], 0.0)
        P.memset(blk32[0:64, 0:64], 1.0)
        P.memset(blk32[64:128, 64:128], 1.0)
        P.copy(R(blk64[:]), blk32[:])
        P.memset(MSK2[:], 1.0)
        P.emit('pool', lambda e: e.affine_select(MSK2[:, 0, :], MSK2[:, 0, :], [[1, 128]], ALU.is_gt, 0.0, base=0,
                                                 channel_multiplier=-1), [MSK2[:]], [MSK2[:]])
        P.emit('pool', lambda e: e.affine_select(MSK2[:, 1, :], MSK2[:, 1, :], [[1, 128]], ALU.is_ge, 0.0, base=0,
                                                 channel_multiplier=-1), [MSK2[:]], [MSK2[:]])
        P.memset(MLS[:], 1.0)
        P.emit('pool', lambda e: e.affine_select(MLS[:], MLS[:], [[-1, 128]], ALU.is_gt, 0.0, base=0,
                                                 channel_multiplier=1), [MLS[:]], [MLS[:]])
        P.ts(NMSK2[:, 0, :], MSK2[:, 0, :], -1.0)
        P.copy(NMSK2[:, 1, :], MSK2[:, 1, :])
        P.memset(cst[:, 0:1], 1e-6); P.memset(cst[:, 1:2], 1.0); P.memset(cst[:, 2:3], 64e-5); P.memset(cst[:, 3:4], 0.0)
        P.memset(SEL32[:], 1.0)
        for h in range(4):
            P.emit('pool', lambda e, h=h: e.affine_select(SEL32[:, h, :], SEL32[:, h, :], [[0, 128]], ALU.is_equal, 0.0,
                                                          base=-(4 + h), channel_multiplier=1), [SEL32[:]], [SEL32[:]])
        P.copy(SEL[:], SEL32[:])

        def colvec(name, ap1d, F):
            t = sb(name, [128, F // 128])
            P.dma(t[:], ap1d.rearrange("(c p) -> p c", p=128), slow=True)
            return t

        nmix = [colvec("nmix%d" % l, D['norm_mix'][l], 1024) for l in range(4)]
        nmlp = [colvec("nmlp%d" % l, D['norm_mlp'][l], 1024) for l in range(4)]
        nfin = colvec("nfin", D['norm_final'], 1024)
        EV = []
        for i in range(2):
            e = {}
            e['mu'] = colvec("mu%d" % i, D['rwkv_mu'][i], 1792)
            for nm in ('w0', 'a0', 'k_k', 'k_a', 'r_k', 'ln_w', 'ln_b'):
                e[nm] = colvec("%s%d" % (nm, i), D['rwkv_' + nm][i], 512)
            e['omka'] = sb("omka%d" % i, [128, 4])
            P.ts(e['omka'][:], e['k_a'][:], -1.0, 1.0, ALU.mult, ALU.add)
            cw = sb("gcw%d" % i, [128, 4, 12])
            for j in range(4):
                P.dma(cw[:, j, :], D['gdn_conv_w'][i, j].rearrange("(c p) -> p c", p=128), slow=True)
            e['gcw'] = cw
            e['gnw'] = colvec("gnw%d" % i, D['gdn_norm_w'][i], 128)
            dtb = sb("dtb%d" % i, [8, 1]); nA = sb("nA%d" % i, [8, 1])
            P.memset(dtb[:], 0.0); P.memset(nA[:], 0.0)
            P.dma(dtb[4:8, :], D['gdn_dt_bias'][i].rearrange("(h o) -> h o", o=1), slow=True)
            P.dma(nA[4:8, :], D['gdn_A_log'][i].rearrange("(h o) -> h o", o=1), slow=True)
            P.act(nA[:], nA[:], AF.Exp)
            P.ts(nA[:], nA[:], -1.0)
            e['dtb'] = dtb; e['nA'] = nA
            e['PREV'] = sb("PREV%d" % i, [128, 14, 16])
            e['HISTG'] = sb("HISTG%d" % i, [128, 12, 3])
            e['HR'] = sb("HR%d" % i, [128, 4, 64])
            e['HG'] = sb("HG%d" % i, [128, 4, 128])
            P.memset(e['PREV'][:], 0.0); P.memset(e['HISTG'][:], 0.0)
            P.copy(R(e['HR'][:]), zeros32[:, 0:256].rearrange('p (a b) -> p a b', a=4)); P.copy(R(e['HG'][:]), zeros32[:, 0:512].rearrange('p (a b) -> p a b', a=4))
            EV.append(e)
        OD = []
        for i in range(2):
            o = {}
            cw = sb("lcw%d" % i, [128, 4, 8])
            for j in range(4):
                P.dma(cw[:, j, :], D['lru_conv_w'][i, j].rearrange("(c p) -> p c", p=128), slow=True)
            o['lcw'] = cw
            o['lcb'] = colvec("lcb%d" % i, D['lru_conv_b'][i], 1024)
            o['ba'] = colvec("lba%d" % i, D['lru_ba'][i], 1024)
            o['bi'] = colvec("lbi%d" % i, D['lru_bi'][i], 1024)
            LL = colvec("lL%d" % i, D['lru_L'][i], 1024)
            P.act(LL[:], LL[:], AF.Exp, scale=-1.0)
            P.act(LL[:], LL[:], AF.Ln, bias=cst[:, 1:2])
            P.ts(LL[:], LL[:], -8.0)
            o['nsp'] = LL
            o['HISTL'] = sb("HISTL%d" % i, [128, 8, 3])
            o['HL'] = sb("HL%d" % i, [128, 8, 16])
            P.memset(o['HISTL'][:], 0.0); P.memset(o['HL'][:], 0.0)
            OD.append(o)

        VT = sb("VT", [128, 512]); BTT = sb("BTT", [128, 512]); KTT = sb("KTT", [128, 512])
        YO = sb("YO", [128, 512]); UU = sb("UU", [128, 512]); YN = sb("YN", [128, 512]); SQT = sb("SQT", [128, 512])
        M1 = [sb("M1_%d" % i, [128, 2, 128]) for i in range(2)]
        M2 = [sb("M2_%d" % i, [128, 2, 128]) for i in range(2)]
        EB2 = [sb("EB2_%d" % i, [128, 2, 128]) for i in range(2)]
        YB = [[sb("Y_%d_%d" % (i, j), [128, 128]) for j in range(2)] for i in range(2)]
        YTB = [[sb("YT_%d_%d" % (i, j), [128, 128]) for j in range(2)] for i in range(2)]
        XB = [[sb("XB_%d_%d" % (i, j), [128, 128]) for j in range(2)] for i in range(2)]
        N0B = [sb("N0_%d" % i, [128, 128]) for i in range(2)]
        ST8 = sb("ST8", [128, 64])
        TP = sb("TP", [128, 16]); PCOL = sb("PCOL", [128, 8])
        DIFF = [sb("DIFF%d" % i, [128, 128]) for i in range(2)]
        OT = [sb("OT%d" % i, [128, 128]) for i in range(2)]
        TQ = [sb("TQ%d" % i, [128, 128]) for i in range(2)]
        KW = [sb("KW%d" % i, [128, 128]) for i in range(2)]
        PCS = sb("PCS", [128, 4])
        G8 = sb("G8", [8, 264]); SIG8 = sb("SIG8", [8, 264]); GC8 = sb("GC8", [8, 128]); BG = sb("BG", [8, 264])
        SEQH = sb("SEQH", [128, 8, 64])
        TMPC = sb("TMPC", [128, 12, 48])

        wblocks = []

        def wspec_even(i):
            L = []
            wi = D['w_in_even'][i]
            for c0, n in ((0, 512), (512, 512), (1024, 512), (1536, 256)):
                L.append([(0, wi[:, c0:c0 + n].rearrange("(k p) n -> p k n", p=128), 8, n)])
            for c0, n in ((1792, 512), (2304, 512), (2816, 512), (3328, 512), (3840, 8)):
                L.append([(0, wi[:, c0:c0 + n].rearrange("(k p) n -> p k n", p=128), 8, n)])
            wo = D['w_out_even'][i]
            for c0 in (0, 512):
                L.append([(0, wo[:, c0:c0 + 512].rearrange("(k p) n -> p k n", p=128), 8, 512)])
            return L

        def wspec_odd(i):
            L = []
            wi = D['w_in_odd'][i]
            for c0 in (0, 512, 1024, 1536):
                L.append([(0, wi[:, c0:c0 + 512].rearrange("(k p) n -> p k n", p=128), 8, 512)])
            wo = D['w_out_odd'][i]
            for c0 in (0, 512):
                L.append([(0, wo[:, c0:c0 + 512].rearrange("(k p) n -> p k n", p=128), 8, 512)])
            return L

        def wspec_mlp(l):
            L = []
            up = D['mlp_up'][l]
            for c0 in range(0, 4096, 512):
                L.append([(0, up[:, c0:c0 + 512].rearrange("(k p) n -> p k n", p=128), 8, 512)])
            dn = D['mlp_down'][l]
            for r0 in range(0, 4096, 512):
                L.append([(0, dn[r0:r0 + 512, :].rearrange("(k p) n -> p k n", p=128), 4, 1024)])
            return L

        for t in range(NTP + 1):
            for l in range(4):
                wblocks.extend(wspec_even(l // 2) if l % 2 == 0 else wspec_odd(l // 2))
                wblocks.extend(wspec_mlp(l))
        wstate = {'issued': 0, 'pos': 0}

        def wissue(upto):
            while wstate['issued'] <= min(upto, len(wblocks) - 1):
                bi = wstate['issued']
                slot = WR[bi % NW]
                for off, src, nk, n in wblocks[bi]:
                    P.dma(R(slot[:, off:off + nk * n].rearrange("p (k n) -> p k n", k=nk)), src, q='pool', cast=True)
                wstate['issued'] += 1

        def wnext():
            bi = wstate['pos']
            wissue(bi + NW - 1)
            wstate['pos'] += 1
            off, src, nk, n = wblocks[bi][0]
            return WR[bi % NW][:, 0:nk * n].rearrange("p (k n) -> p k n", k=nk)

        def tok2feat(dst_fn, rows_ap, n, F, view=None):
            P.dma(STGI[:n, 0:F], rows_ap)
            for b in range(F // 128):
                ps = nps()
                P.tr(ps[:, 0:n], STGI[:n, b * 128:(b + 1) * 128], ident[:n, :n])
                src = ps[:, 0:n]
                if view is not None:
                    src = view(src)
                evac(dst_fn(b), src)

        def feat2tok(src_fn, rows_ap, n, F, is_out=True, stg=None, contig=None):
            stg = STGO if stg is None else stg
            nb = F // 128
            if contig is not None:
                src0 = src_fn
                for b in range(nb):
                    P.copy(contig(TMPC[:, b, 0:n]), src0(b), eng='dve')
                src_fn = lambda b: TMPC[:, b, 0:n]
            for b0 in range(0, nb, 4):
                ps = nps()
                m = min(4, nb - b0)
                for b in range(b0, b0 + m):
                    P.tr(ps[:n, (b - b0) * 128:(b - b0 + 1) * 128], src_fn(b), ident[:])
                evac(stg[:n, b0 * 128:(b0 + m) * 128], ps[:n, 0:m * 128])
            P.dma(rows_ap, stg[:n, 0:F], q='sp', is_out=is_out)

        def rmsnorm(wcol, TT, rnd=True, dst=None):
            P.act(R(XN[:, :, 0:TT]), X[:, :, 0:TT], AF.Square)
            ps = nps()
            for k in range(8):
                P.mm(ps[:, 0:TT], ones[:], XN[:, k, 0:TT], start=(k == 0), stop=(k == 7), r32=True)
            P.act(RS[:, 0:TT], ps[:, 0:TT], AF.Sqrt, bias=cst[:, 0:1], scale=1.0 / 1024)
            P.recip(RS[:, 0:TT], RS[:, 0:TT])
            for k in range(8):
                o = XN[:, k, 0:TT] if dst is None else dst(k)
                P.stt(R(o) if rnd else o, X[:, k, 0:TT], wcol[:, k:k + 1], RS[:, 0:TT], ALU.mult, ALU.mult)

        def project(w, ncols, TT, dest, M=128):
            for j in range((ncols + M - 1) // M):
                m = min(M, ncols - j * M)
                ps = nps()
                for k in range(8):
                    P.mm(ps[:m, 0:TT], w[:, k, j * M:j * M + m], XN[:, k, 0:TT], start=(k == 0), stop=(k == 7), r32=(m % 2 == 0 and TT % 2 == 0))
                dest(j, ps[:m, 0:TT])

        def vw(ap2d, cfg):
            if cfg.NS == 1:
                return ap2d
            return ap2d.rearrange("p (s t) -> p s t", t=cfg.T)

        def xe_views(buf, cc, cfg):
            if cfg.NS == 1:
                base = buf[:, cc, 0:3 + cfg.T]
                return (base[:, 3:3 + cfg.T], lambda j: base[:, j:j + cfg.T], base[:, 0:3], base[:, cfg.T:cfg.T + 3])
            base = buf[:, cc, 0:16 * 11].rearrange("p (s u) -> p s u", u=11)
            return (base[:, :, 3:11], lambda j: base[:, :, j:j + 8], base[:, :, 0:3], base[:, :, 8:11])

        def neumann(par, C, L, N0, NT, X0, Dw):
            Y, YT, Xc = N0, NT, X0
            for k in range(L):
                ps = nps()
                P.mm(ps[:C, 0:Dw], YT[:C, 0:C], Xc[:C, 0:Dw])
                Xn = XB[par][(k + 1) % 2]
                P.tt(R(Xn[:C, 0:Dw]), ps[:C, 0:Dw], Xc[:C, 0:Dw], ALU.add)
                if k < L - 1:
                    ps2 = nps()
                    P.mm(ps2[:C, 0:C], Y[:C, 0:C], YT[:C, 0:C])
                    YTn = YTB[par][(k + 1) % 2]
                    evac(R(YTn[:C, 0:C]), ps2[:C, 0:C])
                    if k < L - 2:
                        ps3 = nps()
                        P.mm(ps3[:C, 0:C], YT[:C, 0:C], Y[:C, 0:C])
                        Yn = YB[par][(k + 1) % 2]
                        evac(R(Yn[:C, 0:C]), ps3[:C, 0:C])
                        Y = Yn
                    YT = YTn
                Xc = Xn
            return Xc

        def rwkv_part(i, cfg):
            e = EV[i]
            TT, NS, T, C, L = cfg.TT, cfg.NS, cfg.T, cfg.C, cfg.L
            P.dma(R(LW[0:64, 0, :]), D['rwkv_w2'][i]); P.dma(R(LW[64:128, 1, :]), D['rwkv_a2'][i]); P.dma(R(LW[:, 2, :]), D['rwkv_g2'][i])
            if cfg.kind == 's':
                tok2feat(lambda b: e['PREV'][:, b, 0:16], D['st_shift'][i], 16, 1792)

            def dest_for(blk):
                def dest(j, ps):
                    oc = blk * 4 + j
                    if oc < 12:
                        evac(B[oc // 4][:, oc % 4, 0:TT], ps)
                    else:
                        evac(B[9][:, oc - 12, 0:TT], ps)
                return dest
            for blk, n in enumerate((512, 512, 512, 256)):
                w = wnext()
                project(w, n, TT, dest_for(blk))
            srcs = [(B[0], c, B[3], c, c) for c in range(4)] + [(B[1], c, B[4], c, 4 + c) for c in range(4)] + \
                   [(B[2], c, B[5], c, 8 + c) for c in range(4)] + [(B[9], 0, B[6], 0, 12), (B[9], 1, B[6], 1, 13)]
            for (pb, pc, db, dc, oc) in srcs:
                p2 = vw(pb[:, pc, 0:TT], cfg)
                d2 = vw(db[:, dc, 0:TT], cfg)
                if NS == 1:
                    P.tt(d2[:, 1:T], p2[:, 0:T - 1], p2[:, 1:T], ALU.subtract)
                    P.tt(d2[:, 0:1], e['PREV'][:, oc, 0:1], p2[:, 0:1], ALU.subtract)
                    P.copy(e['PREV'][:, oc, 0:1], p2[:, T - 1:T], eng='act')
                else:
                    P.tt(d2[:, :, 1:T], p2[:, :, 0:T - 1], p2[:, :, 1:T], ALU.subtract)
                    P.tt(d2[:, :, 0], e['PREV'][:, oc, 0:16], p2[:, :, 0], ALU.subtract)
                    P.copy(e['PREV'][:, oc, 0:16], p2[:, :, T - 1], eng='act')
                P.stt(pb[:, pc, 0:TT], db[:, dc, 0:TT], e['mu'][:, oc:oc + 1], pb[:, pc, 0:TT], ALU.mult, ALU.add)
            if cfg.last:
                P.dma(D['shift_p'][i].rearrange("(c p) -> p c", p=128), e['PREV'][:, :, 0], q='sp', is_out=True, slow=True)
            if cfg.kind == 's':
                feat2tok(lambda b: e['PREV'][:, b, 0:16], D['shift_s'][i], 16, 1792, stg=STGI)
            P.act(R(B[9][0:64, 2, 0:TT]), B[9][0:64, 0, 0:TT], AF.Tanh)
            P.act(R(B[9][64:128, 2, 0:TT]), B[9][64:128, 0, 0:TT], AF.Copy)
            P.act(R(B[9][:, 3, 0:TT]), B[9][:, 1, 0:TT], AF.Sigmoid)
            for c in range(4):
                cs = slice(c * 128, (c + 1) * 128)
                ps = nps()
                P.mm(ps[:, 0:TT], LW[0:64, 0, cs], B[9][0:64, 2, 0:TT])
                P.act(B[3][:, c, 0:TT], ps[:, 0:TT], AF.Sigmoid, bias=e['w0'][:, c:c + 1])
                ps = nps()
                P.mm(ps[:, 0:TT], LW[64:128, 1, cs], B[9][64:128, 2, 0:TT])
                P.act(B[4][:, c, 0:TT], ps[:, 0:TT], AF.Sigmoid, bias=e['a0'][:, c:c + 1])
                ps = nps()
                P.mm(ps[:, 0:TT], LW[:, 2, cs], B[9][:, 3, 0:TT])
                evac(B[6][:, c, 0:TT], ps[:, 0:TT])
                P.ts(B[5][:, c, 0:TT], B[1][:, c, 0:TT], e['k_k'][:, c:c + 1])
                P.act(R(B[7][:, c, 0:TT]), B[5][:, c, 0:TT], AF.Square)
                ps = nps()
                P.mm(ps[:, 0:TT], blk64[:], B[7][:, c, 0:TT])
                P.act(B[7][:, c, 0:TT], ps[:, 0:TT], AF.Sqrt, bias=cst[:, 0:1])
                P.recip(B[7][:, c, 0:TT], B[7][:, c, 0:TT])
                P.tt(B[5][:, c, 0:TT], B[5][:, c, 0:TT], B[7][:, c, 0:TT], ALU.mult)
                P.ts(B[7][:, c, 0:TT], B[4][:, c, 0:TT], e['k_a'][:, c:c + 1], e['omka'][:, c:c + 1], ALU.mult, ALU.add)
                P.tt(B[1][:, c, 0:TT], B[1][:, c, 0:TT], B[7][:, c, 0:TT], ALU.mult)
                P.tt(B[4][:, c, 0:TT], B[5][:, c, 0:TT], B[4][:, c, 0:TT], ALU.mult)
                P.stt(R(B[7][:, c, 0:TT]), B[0][:, c, 0:TT], e['r_k'][:, c:c + 1], B[1][:, c, 0:TT], ALU.mult, ALU.mult)
                ps = nps()
                P.mm(ps[:, 0:TT], blk64[:], B[7][:, c, 0:TT])
                P.tt(B[7][:, c, 0:TT], ps[:, 0:TT], B[2][:, c, 0:TT], ALU.mult)
            HRt = e['HR']
            Gc = B[10][:, :, 0:128]; EX = B[10][:, :, 128:256]
            EXI = B[11][:, :, 0:128]; EXE = B[11][:, :, 128:256]
            AR = WG
            BT = SEQH
            for ci, (c0, C) in enumerate(cfg.chunks):
                cs = slice(c0, c0 + C)
                if cfg.kind == 's':
                    P.dma(STGI[0:64, 0:512].rearrange("v (h k) -> v h k", h=8), D['st_rwkv'][i, ci].rearrange("h v k -> v h k"))
                    for j in range(4):
                        ps = nps()
                        P.tr(ps[:, 0:64], STGI[0:64, j * 128:(j + 1) * 128], ident[0:64, 0:64])
                        evac(R(HRt[:, j, :]), ps[:, 0:64])
                for j in range(4):
                    P.scan(Gc[:, j, 0:C], ones32[:, 0:C], B[3][:, j, cs], 0.0, ALU.mult, ALU.add)
                P.act(EX[:, :, 0:C], Gc[:, :, 0:C], AF.Exp, scale=-DECAY_C)
                P.act(EXI[:, :, 0:C], Gc[:, :, 0:C], AF.Exp, scale=DECAY_C)
                P.tt(EXE[:, :, 0:C], Gc[:, :, 0:C], B[3][:, :, cs], ALU.subtract)
                P.act(EXE[:, :, 0:C], EXE[:, :, 0:C], AF.Exp, scale=-DECAY_C)
                P.stt(R(WG[:, 0, 0:4, 0:C]), B[5][:, :, cs], -1.0, EXE[:, :, 0:C], ALU.mult, ALU.mult)
                P.tt(R(WG[:, 1, 0:4, 0:C]), B[0][:, :, cs], EX[:, :, 0:C], ALU.mult)
                BTl = WG[:, 0, 4:8, :]; KTl = WG[:, 1, 4:8, :]
                P.tt(R(BTl[:, :, 0:C]), B[4][:, :, cs], EXI[:, :, 0:C], ALU.mult)
                P.tt(R(KTl[:, :, 0:C]), B[1][:, :, cs], EXI[:, :, 0:C], ALU.mult)
                for (src_fn, dstt) in ((lambda j: B[2][:, j, cs], VT), (lambda j: BTl[:, j, 0:C], BTT), (lambda j: KTl[:, j, 0:C], KTT)):
                    ps = nps()
                    for j in range(4):
                        P.tr(ps[:C, j * 128:(j + 1) * 128], src_fn(j), ident[:])
                    evac(R(dstt[:C, :]), ps[:C, :])
                for h in range(8):
                    j, pb = h // 2, (h % 2) * 64
                    rows = slice(pb, pb + 64)
                    par = h % 2
                    hs = slice(h * 64, (h + 1) * 64)
                    at = WG[rows, 0, j, 0:C]; rt = WG[rows, 1, j, 0:C]
                    ps1 = nps()
                    P.mm(ps1[:C, 0:C], BTl[rows, j, 0:C], at)
                    P.mm(ps1[:C, 128:128 + C], BTl[rows, j, 0:C], rt)
                    ps2 = nps()
                    P.mm(ps2[:C, 0:C], KTl[rows, j, 0:C], at)
                    P.mm(ps2[:C, 128:128 + C], KTl[rows, j, 0:C], rt)
                    ps3 = nps()
                    P.mm(ps3[:C, 0:C], at, BTl[rows, j, 0:C])
                    m1, m2, n0 = M1[par], M2[par], N0B[par]
                    P.tt(R(m1[:C, :, 0:C]), ps1[:C, 0:256].rearrange("p (a b) -> p a b", a=2)[:, :, 0:C], MSK2[:C, :, 0:C], ALU.mult)
                    P.tt(R(m2[:C, :, 0:C]), ps2[:C, 0:256].rearrange("p (a b) -> p a b", a=2)[:, :, 0:C], MSK2[:C, :, 0:C], ALU.mult)
                    P.tt(R(n0[:C, 0:C]), ps3[:C, 0:C], MLS[:C, 0:C], ALU.mult)
                    psr = nps()
                    P.mm(psr[:C, 0:64], at, HRt[rows, j, :], start=True, stop=False)
                    P.mm(psr[:C, 0:64], m2[:C, 0, 0:C], VT[:C, hs], start=False, stop=True)
                    x0 = XB[par][0]
                    evac(R(x0[:C, 0:64]), psr[:C, 0:64])
                    U = neumann(par, C, L, n0, m1[:, 0, :], x0, 64)
                    P.copy(R(UU[:C, hs]), U[:C, 0:64], eng='dve')
                    psy = nps()
                    P.mm(psy[:C, 0:64], rt, HRt[rows, j, :], start=True, stop=False)
                    P.mm(psy[:C, 0:64], m1[:C, 1, 0:C], U[:C, 0:64], start=False, stop=False)
                    P.mm(psy[:C, 0:64], m2[:C, 1, 0:C], VT[:C, hs], start=False, stop=True)
                    evac(YO[:C, hs], psy[:C, 0:64])
                for j in range(4):
                    pc = slice(j * 128, (j + 1) * 128)
                    pss = nps()
                    P.mm(pss[:, 0:128], BTT[:C, pc], UU[:C, pc], start=True, stop=False)
                    P.mm(pss[:, 0:128], KTT[:C, pc], VT[:C, pc], start=False, stop=True)
                    for hb in range(2):
                        rows = slice(hb * 64, hb * 64 + 64)
                        P.tt(SEQH[rows, j, :], HRt[rows, j, :], pss[rows, hb * 64:hb * 64 + 64], ALU.add)
                        P.ts(R(HRt[rows, j, :]), SEQH[rows, j, :], EX[rows, j, C - 1:C])
                if cfg.kind == 's' or cfg.last and ci == len(cfg.chunks) - 1:
                    for j in range(4):
                        ps = nps()
                        P.tr(ps[0:64, 0:128], HRt[:, j, :], ident[:])
                        evac(STGO[0:64, j * 128:(j + 1) * 128], ps[0:64, 0:128])
                    dst = D['rwkv_s'][i, ci] if cfg.kind == 's' else D['rwkv_p'][i]
                    P.dma(dst.rearrange("h v k -> v h k"), STGO[0:64, 0:512].rearrange("v (h k) -> v h k", h=8), q='sp', is_out=True)
                P.reduce(ST8[:C, 0:8], YO[:C, :].rearrange("p (h d) -> p h d", h=8))
                P.act(SQT[:C, :], YO[:C, :], AF.Square)
                P.reduce(ST8[:C, 8:16], SQT[:C, :].rearrange("p (h d) -> p h d", h=8))
                P.ts(ST8[:C, 16:24], ST8[:C, 0:8], 1.0 / 64)
                P.tt(ST8[:C, 24:32], ST8[:C, 16:24], ST8[:C, 16:24], ALU.mult)
                P.stt(ST8[:C, 32:40], ST8[:C, 8:16], 1.0 / 64, ST8[:C, 24:32], ALU.mult, ALU.subtract)
                P.act(ST8[:C, 40:48], ST8[:C, 32:40], AF.Sqrt, bias=cst[:C, 2:3])
                P.recip(ST8[:C, 48:56], ST8[:C, 40:48])
                for h in range(8):
                    hs = slice(h * 64, (h + 1) * 64)
                    P.ts(YN[:C, hs], YO[:C, hs], ST8[:C, 16 + h:17 + h], ST8[:C, 48 + h:49 + h], ALU.subtract, ALU.mult)
                ps = nps()
                for j in range(4):
                    P.tr(ps[:, j * 128:j * 128 + C], YN[:C, j * 128:(j + 1) * 128], ident[:C, :C])
                for j in range(4):
                    P.ts(B[8][:, j, cs], ps[:, j * 128:j * 128 + C], e['ln_w'][:, j:j + 1], e['ln_b'][:, j:j + 1], ALU.mult, ALU.add)
            P.tt(B[8][:, :, 0:TT], B[8][:, :, 0:TT], B[7][:, :, 0:TT], ALU.add, eng='dve')
            P.tt(R(B[8][:, :, 0:TT]), B[8][:, :, 0:TT], B[6][:, :, 0:TT], ALU.mult, eng='dve')

        def gdn_part(i, cfg):
            e = EV[i]
            TT, NS, T, C, L = cfg.TT, cfg.NS, cfg.T, cfg.C, cfg.L
            HGt = e['HG']
            if cfg.kind == 's':
                tok2feat(lambda b: xe_views(B[b // 4], b % 4, cfg)[2], D['st_gconv'][i].rearrange("s j f -> (s j) f"), 48, 1536,
                         view=lambda a: a.rearrange("p (s j) -> p s j", j=3))
            else:
                for c in range(12):
                    P.copy(xe_views(B[c // 4], c % 4, cfg)[2], e['HISTG'][:, c, :], eng='dve')

            def dest_qkv(blk):
                def dest(j, ps):
                    c = blk * 4 + j
                    evac(xe_views(B[c // 4], c % 4, cfg)[0], vw(ps, cfg))
                return dest
            for blk in range(3):
                w = wnext()
                project(w, 512, TT, dest_qkv(blk))
            w = wnext()
            project(w, 512, TT, lambda j, ps: evac(B[6][:, j, 0:TT], ps))
            w = wnext()
            project(w, 8, TT, lambda j, ps: evac(BG[:, 0:TT], ps), M=8)
            if cfg.kind == 's':
                feat2tok(lambda b: xe_views(B[b // 4], b % 4, cfg)[3], D['gconv_s'][i].rearrange("s j f -> (s j) f"), 48, 1536,
                         contig=lambda a: a.rearrange("p (s j) -> p s j", j=3))
            else:
                for c in range(12):
                    P.copy(e['HISTG'][:, c, :], xe_views(B[c // 4], c % 4, cfg)[3], eng='dve')
                if cfg.last:
                    feat2tok(lambda b: e['HISTG'][:, b, :], D['gconv_p'][i], 3, 1536)
            for c in range(12):
                _, tap, _, _ = xe_views(B[c // 4], c % 4, cfg)
                ob = B[3 + c // 4]
                o2 = ob[:, c % 4, 0:TT]
                o = vw(o2, cfg)
                P.ts(o, tap(0), e['gcw'][:, 0, c:c + 1])
                for jt in range(1, 4):
                    P.stt(o, tap(jt), e['gcw'][:, jt, c:c + 1], o, ALU.mult, ALU.add)
                P.act(o2, o2, AF.Silu)
                if c < 8:
                    P.act(R(B[7][:, c % 4, 0:TT]), o2, AF.Square)
                    ps = nps()
                    P.mm(ps[:, 0:TT], ones[:], B[7][:, c % 4, 0:TT])
                    P.act(B[7][:, c % 4, 0:TT], ps[:, 0:TT], AF.Sqrt, bias=cst[:, 0:1])
                    P.recip(B[7][:, c % 4, 0:TT], B[7][:, c % 4, 0:TT])
                    P.stt(R(o2), o2, (128.0 ** -0.5) if c < 4 else 1.0, B[7][:, c % 4, 0:TT], ALU.mult, ALU.mult)
            P.act(SIG8[:, 0:TT], BG[:, 0:TT], AF.Sigmoid)
            P.act(G8[:, 0:TT], BG[:, 0:TT], AF.Exp, bias=e['dtb'][:, 0:1])
            P.act(G8[:, 0:TT], G8[:, 0:TT], AF.Ln, bias=cst[0:8, 1:2])
            P.ts(G8[:, 0:TT], G8[:, 0:TT], e['nA'][:, 0:1])
            P.act(B[6][:, :, 0:TT], B[6][:, :, 0:TT], AF.Silu)
            KQ = WG
            for ci, (c0, C) in enumerate(cfg.chunks):
                cs = slice(c0, c0 + C)
                if cfg.kind == 's':
                    P.dma(R(HGt[:]), D['st_gdn'][i, ci].rearrange("h k v -> k h v"))
                P.scan(GC8[:, 0:C], ones32[0:8, 0:C], G8[:, cs], 0.0, ALU.mult, ALU.add)
                ps = nps()
                P.tr(ps[:C, 0:8], SIG8[:, cs], ident[0:8, 0:8])
                P.tr(ps[:C, 8:16], GC8[:, 0:C], ident[0:8, 0:8])
                evac(TP[:C, 0:16], ps[:C, 0:16])
                P.act(PCOL[:C, 0:4], TP[:C, 12:16], AF.Exp)
                P.ts(PCOL[:C, 4:8], PCOL[:C, 0:4], -1.0)
                psv = nps()
                psk = nps()
                for h in range(4):
                    P.tr(psv[:C, h * 128:(h + 1) * 128], B[5][:, h, cs], ident[:])
                    P.tr(psk[:C, h * 128:(h + 1) * 128], B[4][:, h, cs], ident[:])
                evac(VT[:C, :], psv[:C, :])
                evac(KTT[:C, :], psk[:C, :])
                P.copy(R(KQ[:, 0, 0:4, 0:C]), B[4][:, :, cs], eng='dve')
                P.copy(R(KQ[:, 1, 0:4, 0:C]), B[3][:, :, cs], eng='dve')
                for h in range(4):
                    par = h % 2
                    hs = slice(h * 128, (h + 1) * 128)
                    kT = KQ[:, 0, h, 0:C]; qT = KQ[:, 1, h, 0:C]
                    psg = nps()
                    P.mm(psg[:C, 0:C], SEL[0:8, h, 0:C], GC8[0:8, 0:C], r32=False)
                    df = DIFF[par]
                    P.ts(df[:C, 0:C], psg[:C, 0:C], TP[:C, 12 + h:13 + h], 0.0, ALU.subtract, ALU.min)
                    P.act(df[:C, 0:C], df[:C, 0:C], AF.Exp)
                    eb = EB2[par]
                    P.stt(eb[:C, 0, 0:C], df[:C, 0:C], TP[:C, h:h + 1], NMSK2[:C, 0, 0:C], ALU.mult, ALU.mult)
                    P.stt(eb[:C, 1, 0:C], df[:C, 0:C], TP[:C, h:h + 1], MSK2[:C, 1, 0:C], ALU.mult, ALU.mult)
                    pskk = nps()
                    P.mm(pskk[:C, 0:C], kT, kT)
                    P.mm(pskk[:C, 128:128 + C], kT, qT)
                    at2 = M1[par]
                    P.tt(R(at2[:C, :, 0:C]), pskk[:C, 0:256].rearrange("p (a b) -> p a b", a=2)[:, :, 0:C], eb[:C, :, 0:C], ALU.mult)
                    n0 = N0B[par]
                    pst = nps()
                    P.tr(pst[:C, 0:C], at2[:C, 0, 0:C], ident[:C, :C])
                    evac(R(n0[:C, 0:C]), pst[:C, 0:C])
                    pskh = nps()
                    P.mm(pskh[:C, 0:128], kT, HGt[:, h, :])
                    x0 = XB[par][0]
                    P.stt(R(x0[:C, 0:128]), pskh[:C, 0:128], PCOL[:C, 4 + h:5 + h], VT[:C, hs], ALU.mult, ALU.add)
                    Wn = neumann(par, C, L, n0, at2[:, 0, :], x0, 128)
                    psq = nps()
                    P.mm(psq[:C, 0:128], qT, HGt[:, h, :])
                    tq = TQ[par]
                    P.act(tq[:C, :], psq[:C, 0:128], AF.Copy, scale=PCOL[:C, h:h + 1])
                    pso = nps()
                    P.mm(pso[:C, 0:128], at2[:C, 1, 0:C], Wn[:C, 0:128])
                    ot = OT[par]
                    P.tt(ot[:C, :], pso[:C, 0:128], tq[:C, :], ALU.add)
                    P.act(tq[:C, :], ot[:C, :], AF.Square, accum_out=ST8[:C, 56 + h:57 + h])
                    P.act(ST8[:C, 60:61], ST8[:C, 56 + h:57 + h], AF.Sqrt, bias=cst[:C, 0:1], scale=1.0 / 128)
                    P.recip(ST8[:C, 61:62], ST8[:C, 60:61])
                    P.ts(ot[:C, :], ot[:C, :], ST8[:C, 61:62])
                    pso2 = nps()
                    P.tr(pso2[:, 0:C], ot[:C, :], ident[:C, :C])
                    P.stt(R(B[6][:, h, cs]), pso2[:, 0:C], e['gnw'][:, 0:1], B[6][:, h, cs], ALU.mult, ALU.mult)
                    kw = KW[par]
                    P.ts(R(kw[:C, :]), KTT[:C, hs], eb[:C, 1, C - 1:C])
                    pspc = nps()
                    P.mm(pspc[:, 0:2], SEL[0:8, h, :], GC8[0:8, C - 2:C], r32=False)
                    P.act(PCS[:, h:h + 1], pspc[:, 1:2], AF.Exp)
                    psst = nps()
                    P.mm(psst[:, 0:128], kw[:C, :], Wn[:C, 0:128])
                    P.stt(R(HGt[:, h, :]), HGt[:, h, :], PCS[:, h:h + 1], psst[:, 0:128], ALU.mult, ALU.add)
                if cfg.kind == 's':
                    P.dma(D['gdn_s'][i, ci].rearrange("h k v -> k h v"), HGt[:], q='sp', is_out=True)
                elif cfg.last and ci == len(cfg.chunks) - 1:
                    P.dma(D['gdn_p'][i].rearrange("h k v -> k h v"), HGt[:], q='sp', is_out=True)

        def out_proj(chunks, TT):
            for blk in range(2):
                w = wnext()
                for j in range(4):
                    oc = blk * 4 + j
                    ps = nps()
                    for k in range(8):
                        P.mm(ps[:, 0:TT], w[:, k, j * 128:(j + 1) * 128], chunks[k], start=(k == 0), stop=(k == 7))
                    P.tt(X[:, oc, 0:TT], X[:, oc, 0:TT], ps[:, 0:TT], ALU.add)

        def even_layer(l, cfg):
            i = l // 2
            TT = cfg.TT
            rmsnorm(nmix[l], TT)
            rwkv_part(i, cfg)
            gdn_part(i, cfg)
            out_proj([B[8][:, c, 0:TT] for c in range(4)] + [B[6][:, c, 0:TT] for c in range(4)], TT)

        def odd_layer(l, cfg):
            i = l // 2
            o = OD[i]
            TT, NS, T = cfg.TT, cfg.NS, cfg.T
            rmsnorm(nmix[l], TT)
            P.dma(R(WG[:, 0, :, :]), D['lru_wa'][i].rearrange("h a b -> a h b"))
            P.dma(R(WG[:, 1, :, :]), D['lru_wi'][i].rearrange("h a b -> a h b"))
            if cfg.kind == 's':
                tok2feat(lambda b: xe_views(B[2 + b // 4], b % 4, cfg)[2], D['st_lconv'][i].rearrange("s j f -> (s j) f"), 48, 1024,
                         view=lambda a: a.rearrange("p (s j) -> p s j", j=3))
                tok2feat(lambda b: o['HL'][:, b, 0:16], D['st_lru'][i], 16, 1024)
            else:
                for c in range(8):
                    P.copy(xe_views(B[2 + c // 4], c % 4, cfg)[2], o['HISTL'][:, c, :], eng='dve')

            def dest_for(blk):
                def dest(j, ps):
                    c = blk * 4 + j
                    if c < 8:
                        evac(B[c // 4][:, c % 4, 0:TT], ps)
                    else:
                        evac(xe_views(B[2 + (c - 8) // 4], c % 4, cfg)[0], vw(ps, cfg))
                return dest
            for blk in range(4):
                w = wnext()
                project(w, 512, TT, dest_for(blk))
            if cfg.kind == 's':
                feat2tok(lambda b: xe_views(B[2 + b // 4], b % 4, cfg)[3], D['lconv_s'][i].rearrange("s j f -> (s j) f"), 48, 1024,
                         contig=lambda a: a.rearrange("p (s j) -> p s j", j=3))
            else:
                for c in range(8):
                    P.copy(o['HISTL'][:, c, :], xe_views(B[2 + c // 4], c % 4, cfg)[3], eng='dve')
                if cfg.last:
                    feat2tok(lambda b: o['HISTL'][:, b, :], D['lconv_p'][i], 3, 1024)
            for c in range(8):
                _, tap, _, _ = xe_views(B[2 + c // 4], c % 4, cfg)
                xc2 = B[4 + c // 4][:, c % 4, 0:TT]
                xc = vw(xc2, cfg)
                P.ts(xc, tap(0), o['lcw'][:, 0, c:c + 1], o['lcb'][:, c:c + 1], ALU.mult, ALU.add)
                for jt in range(1, 3):
                    P.stt(xc, tap(jt), o['lcw'][:, jt, c:c + 1], xc, ALU.mult, ALU.add)
                P.stt(R(xc), tap(3), o['lcw'][:, 3, c:c + 1], xc, ALU.mult, ALU.add)
                rg = B[6 + c // 4][:, c % 4, 0:TT]
                ig = B[8 + c // 4][:, c % 4, 0:TT]
                tmp = B[10 + c // 4][:, c % 4, 0:TT]
                ps = nps()
                P.mm(ps[:, 0:TT], WG[:, 0, c, :], xc2)
                P.act(rg, ps[:, 0:TT], AF.Sigmoid, bias=o['ba'][:, c:c + 1])
                ps = nps()
                P.mm(ps[:, 0:TT], WG[:, 1, c, :], xc2)
                P.act(ig, ps[:, 0:TT], AF.Sigmoid, bias=o['bi'][:, c:c + 1])
                P.act(rg, rg, AF.Exp, scale=o['nsp'][:, c:c + 1])
                P.act(tmp, rg, AF.Square)
                P.act(tmp, tmp, AF.Sqrt, bias=cst[:, 1:2], scale=-1.0)
                P.tt(ig, ig, tmp, ALU.mult)
                P.tt(ig, ig, xc2, ALU.mult)
                if NS == 1:
                    P.scan(tmp, rg, ig, o['HL'][:, c, 0:1], ALU.mult, ALU.add)
                    P.copy(o['HL'][:, c, 0:1], tmp[:, T - 1:T], eng='act')
                else:
                    for s in range(16):
                        ss = slice(s * 8, s * 8 + 8)
                        P.scan(tmp[:, ss], rg[:, ss], ig[:, ss], o['HL'][:, c, s:s + 1], ALU.mult, ALU.add)
                    P.copy(o['HL'][:, c, 0:16], vw(tmp, cfg)[:, :, T - 1], eng='act')
                g = B[c // 4][:, c % 4, 0:TT]
                P.act(xc2, g, AF.Square)
                P.ts(xc2, xc2, 0.044715, 1.0, ALU.mult, ALU.add)
                P.tt(xc2, xc2, g, ALU.mult)
                P.act(xc2, xc2, AF.Sigmoid, scale=1.5957691216057308)
                P.tt(xc2, xc2, g, ALU.mult)
                P.tt(R(g), xc2, tmp, ALU.mult)
            if cfg.kind == 's':
                feat2tok(lambda b: o['HL'][:, b, 0:16], D['lru_s'][i], 16, 1024)
            elif cfg.last:
                feat2tok(lambda b: o['HL'][:, b, 0:1], D['lru_p'][i].rearrange("(o f) -> o f", o=1), 1, 1024)
            out_proj([B[c // 4][:, c % 4, 0:TT] for c in range(8)], TT)

        def mlp(l, cfg):
            TT = cfg.TT
            rmsnorm(nmlp[l], TT)
            for blk in range(8):
                w = wnext()
                for j in range(4):
                    ff = blk * 4 + j
                    ps = nps()
                    for k in range(8):
                        P.mm(ps[:, 0:TT], w[:, k, j * 128:(j + 1) * 128], XN[:, k, 0:TT], start=(k == 0), stop=(k == 7), r32=True)
                    hb = B[ff // 4][:, ff % 4, 0:TT]
                    P.act(hb, ps[:, 0:TT], AF.Relu)
                    P.tt(R(hb), hb, hb, ALU.mult, eng='dve')
            for rb in range(8):
                w = wnext()
                for kl in range(4):
                    ff = rb * 4 + kl
                    for oc in range(8):
                        P.mm(PS[oc][:, 0:TT], w[:, kl, oc * 128:(oc + 1) * 128], B[ff // 4][:, ff % 4, 0:TT],
                             start=(ff == 0), stop=(ff == 31))
            for oc in range(8):
                P.tt(X[:, oc, 0:TT], X[:, oc, 0:TT], PS[oc][:, 0:TT], ALU.add)

        tiles = [Cfg('p', t) for t in range(NTP)] + [Cfg('s', 0)]
        for cfg in tiles:
            TT = cfg.TT
            xin = D['xp'] if cfg.kind == 'p' else D['xs']
            yout = D['yp'] if cfg.kind == 'p' else D['ys']
            r0 = cfg.idx * TTP if cfg.kind == 'p' else 0
            for q in range(TT // 128):
                P.dma(STGI[:, 0:1024], xin[r0 + q * 128:r0 + (q + 1) * 128, :])
                for hh in range(2):
                    ps = nps()
                    for c in range(4):
                        k = hh * 4 + c
                        P.tr(ps[:, c * 128:(c + 1) * 128], STGI[:, k * 128:(k + 1) * 128], ident[:])
                    evac(X[:, hh * 4:(hh + 1) * 4, q * 128:(q + 1) * 128], ps[:].rearrange("p (c t) -> p c t", c=4))
            for l in range(4):
                if l % 2 == 0:
                    even_layer(l, cfg)
                else:
                    odd_layer(l, cfg)
                mlp(l, cfg)
            rmsnorm(nfin, TT, rnd=False, dst=lambda k: B[k // 4][:, k % 4, 0:TT])
            for q in range(TT // 128):
                feat2tok(lambda b: B[b // 4][:, b % 4, q * 128:(q + 1) * 128], yout[r0 + q * 128:r0 + (q + 1) * 128, :], 128, 1024)
        assert wstate['pos'] == len(wblocks), (wstate['pos'], len(wblocks))
        P.finish()
        print("instructions:", P.ninstr, {k: len(v) for k, v in P.q.items()}, "sems:", len(P.sems))
    return nc


_OUT_ORDER = ['yp', 'ys', 'rwkv_p', 'rwkv_s', 'shift_p', 'shift_s', 'gdn_p', 'gdn_s', 'gconv_p', 'gconv_s',
              'lru_p', 'lru_s', 'lconv_p', 'lconv_s']


def kernel(**inp):
    f = lambda a: np.ascontiguousarray(np.asarray(a, dtype=np.float32))
    shared = {}
    for nm in ('norm_mix', 'norm_mlp', 'norm_final', 'w_in_even', 'w_out_even', 'rwkv_mu', 'rwkv_w0', 'rwkv_w2', 'rwkv_a0',
               'rwkv_a2', 'rwkv_g2', 'rwkv_k_k', 'rwkv_k_a', 'rwkv_ln_w', 'rwkv_ln_b', 'gdn_conv_w', 'gdn_A_log',
               'gdn_dt_bias', 'gdn_norm_w', 'w_in_odd', 'w_out_odd', 'lru_conv_w', 'lru_conv_b', 'lru_wa', 'lru_wi',
               'mlp_up', 'mlp_down'):
        shared[nm] = f(inp[nm])
    shared['rwkv_r_k'] = f(inp['rwkv_r_k']).reshape(2, 512)
    for nm in ('lru_ba', 'lru_bi', 'lru_L'):
        shared[nm] = f(inp[nm]).reshape(2, 1024)
    xp = f(inp['x_prompt']); xs = f(inp['x_sample'])
    stn = {'st_rwkv': 'state_rwkv', 'st_shift': 'state_rwkv_shift', 'st_gdn': 'state_gdn', 'st_gconv': 'state_gdn_conv',
           'st_lru': 'state_lru', 'st_lconv': 'state_lru_conv'}
    in_maps = []
    for c in range(8):
        m = dict(shared)
        m['xp'] = xp[c]
        m['xs'] = np.ascontiguousarray(xs[16 * c:16 * (c + 1)].reshape(128, 1024))
        for k, v in stn.items():
            m[k] = np.ascontiguousarray(f(inp[v])[:, 16 * c:16 * (c + 1)])
        in_maps.append(m)
    nc = build_program()
    res = run_bass_kernel_spmd(nc, in_maps, core_ids=list(range(8)))
    rs = res.results
    g = lambda nm, c: np.asarray(rs[c][nm], dtype=np.float32)
    out = []
    out.append(np.stack([g('yp', c) for c in range(8)], 0))
    out.append(np.concatenate([g('ys', c).reshape(16, 8, 1024) for c in range(8)], 0))
    for nm in ('rwkv', 'shift', 'gdn', 'gconv', 'lru', 'lconv'):
        out.append(np.stack([g(nm + '_p', c) for c in range(8)], 1))
        out.append(np.concatenate([g(nm + '_s', c) for c in range(8)], 1))
    return tuple(out)
```
